# Optimizing a Trainium2 kernel written in Bass

```python
import jax, jax.numpy as jnp
from jax import lax
import numpy as np

D_MODEL = 1024
BATCH = 2
SEQ = 16384
DEPTH = 4

CHUNK = 64
LEFT_CHUNKS = 8
BAND = (LEFT_CHUNKS + 1) * CHUNK
ATT_HEAD_DIM = 64
ATT_HEADS = D_MODEL // 128
ATT_WIDTH = ATT_HEADS * ATT_HEAD_DIM
REL_CLIP = 128
CONV_CH = D_MODEL // 4
CONV_K = 3
SG_WIDTH = D_MODEL // 4
SG_GROUPS = 4
SG_GROUP_DIM = SG_WIDTH // SG_GROUPS
SG_BLOCK = 128
GROUP_DIM = 64
MIX_WIDTH = ATT_WIDTH + CONV_CH + SG_WIDTH
N_GROUPS = MIX_WIDTH // GROUP_DIM
IN_WIDTH = 3 * ATT_WIDTH + 3 * CONV_CH + 2 * SG_WIDTH
D_FF = 4 * D_MODEL
EPS = 1e-6

kernel_name = "hybrid_chunk_attn_gconv_gmlp_trunk"


def rms_norm(x, g):
    xf = x.astype(jnp.float32)
    y = xf * lax.rsqrt(jnp.mean(xf * xf, axis=-1, keepdims=True) + EPS)
    return (y * g.astype(jnp.float32)).astype(x.dtype)


def layer_norm(x, g, b):
    xf = x.astype(jnp.float32)
    mu = jnp.mean(xf, axis=-1, keepdims=True)
    var = jnp.mean(jnp.square(xf - mu), axis=-1, keepdims=True)
    y = (xf - mu) * lax.rsqrt(var + EPS)
    return (y * g.astype(jnp.float32) + b.astype(jnp.float32)).astype(x.dtype)


def chunked_rel_attention(q, k, v, rel_bias):
    b, s, h, dh = q.shape
    n_c = s // CHUNK
    qc = q.reshape(b, n_c, CHUNK, h, dh)
    pad = ((0, 0), (LEFT_CHUNKS, 0), (0, 0), (0, 0), (0, 0))
    kp = jnp.pad(k.reshape(b, n_c, CHUNK, h, dh), pad)
    vp = jnp.pad(v.reshape(b, n_c, CHUNK, h, dh), pad)
    kb = jnp.stack([kp[:, j:j + n_c] for j in range(LEFT_CHUNKS + 1)], axis=2).reshape(b, n_c, BAND, h, dh)
    vb = jnp.stack([vp[:, j:j + n_c] for j in range(LEFT_CHUNKS + 1)], axis=2).reshape(b, n_c, BAND, h, dh)
    rel = jnp.arange(CHUNK)[:, None] + LEFT_CHUNKS * CHUNK - jnp.arange(BAND)[None, :]
    idx = jnp.clip(rel, -REL_CLIP, REL_CLIP) + REL_CLIP
    bias = rel_bias.astype(jnp.float32)[:, idx]
    key_chunk = jnp.arange(n_c)[:, None] - LEFT_CHUNKS + jnp.arange(BAND)[None, :] // CHUNK
    valid = key_chunk >= 0
    scores = jnp.einsum('bnqhd,bnkhd->bnhqk', qc, kb).astype(jnp.float32) * (dh ** -0.5)
    scores = scores + bias[None, None]
    scores = jnp.where(valid[None, :, None, None, :], scores, jnp.float32(-1e30))
    p = jax.nn.softmax(scores, axis=-1).astype(v.dtype)
    out = jnp.einsum('bnhqk,bnkhd->bnqhd', p, vb)
    return out.reshape(b, s, h * dh)


def gated_short_conv(bg, cg, xh, conv_w):
    z = cg * xh
    s = z.shape[1]
    zp = jnp.pad(z, ((0, 0), (CONV_K - 1, 0), (0, 0)))
    y = sum(zp[:, j:j + s] * conv_w[j] for j in range(CONV_K))
    return bg * y


def spatial_gating(u, v, ln_g, ln_b, sg_w, sg_b):
    b, s, _ = u.shape
    u = jax.nn.gelu(u, approximate=False)
    v = layer_norm(jax.nn.gelu(v, approximate=False), ln_g, ln_b)
    vb = v.reshape(b, s // SG_BLOCK, SG_BLOCK, SG_GROUPS, SG_GROUP_DIM)
    mask = jnp.tril(jnp.ones((SG_BLOCK, SG_BLOCK), dtype=sg_w.dtype))
    ws = sg_w * mask[None]
    mixed = jnp.einsum('gts,bnsgc->bntgc', ws, vb) + sg_b.T[None, None, :, :, None]
    return u * mixed.reshape(b, s, SG_WIDTH)


def setup_inputs(seed: int = 0) -> dict:
    key = jax.random.key(seed)
    ks = jax.random.split(key, 16)
    f32 = jnp.float32
    nrm = lambda k, shp, sc: jax.random.normal(k, shp, f32) * sc
    return {
        "x": nrm(ks[0], (BATCH, SEQ, D_MODEL), 1.0),
        "mix_norm_g": 1.0 + nrm(ks[1], (DEPTH, D_MODEL), 0.02),
        "w_in": nrm(ks[2], (DEPTH, D_MODEL, IN_WIDTH), D_MODEL ** -0.5),
        "rel_bias": nrm(ks[3], (DEPTH, ATT_HEADS, 2 * REL_CLIP + 1), 0.1),
        "conv_w": nrm(ks[4], (DEPTH, CONV_K, CONV_CH), CONV_K ** -0.5),
        "sg_ln_g": 1.0 + nrm(ks[5], (DEPTH, SG_WIDTH), 0.02),
        "sg_ln_b": nrm(ks[6], (DEPTH, SG_WIDTH), 0.02),
        "sg_w": nrm(ks[7], (DEPTH, SG_GROUPS, SG_BLOCK, SG_BLOCK), 0.5 * SG_BLOCK ** -0.5),
        "sg_b": 1.0 + nrm(ks[8], (DEPTH, SG_GROUPS, SG_BLOCK), 0.02),
        "group_norm_g": 1.0 + nrm(ks[9], (DEPTH, MIX_WIDTH), 0.02),
        "w_out": nrm(ks[10], (DEPTH, MIX_WIDTH, D_MODEL), 0.5 * MIX_WIDTH ** -0.5),
        "mlp_norm_g": 1.0 + nrm(ks[11], (DEPTH, D_MODEL), 0.02),
        "w_up": nrm(ks[12], (DEPTH, D_MODEL, D_FF), D_MODEL ** -0.5),
        "w_down": nrm(ks[13], (DEPTH, D_FF, D_MODEL), 0.5 * D_FF ** -0.5),
        "final_norm_g": 1.0 + nrm(ks[14], (D_MODEL,), 0.02),
    }


def reference(x, mix_norm_g, w_in, rel_bias, conv_w, sg_ln_g, sg_ln_b, sg_w, sg_b,
              group_norm_g, w_out, mlp_norm_g, w_up, w_down, final_norm_g):
    b, s, _ = x.shape
    cuts = np.cumsum([ATT_WIDTH, ATT_WIDTH, ATT_WIDTH, CONV_CH, CONV_CH, CONV_CH, SG_WIDTH])
    for l in range(DEPTH):
        n = rms_norm(x, mix_norm_g[l])
        proj = jnp.einsum('bsd,de->bse', n, w_in[l])
        q, k, v, bg, cg, xh, su, sv = jnp.split(proj, cuts, axis=-1)
        shp = (b, s, ATT_HEADS, ATT_HEAD_DIM)
        y_a = chunked_rel_attention(q.reshape(shp), k.reshape(shp), v.reshape(shp), rel_bias[l])
        y_b = gated_short_conv(bg, cg, xh, conv_w[l])
        y_c = spatial_gating(su, sv, sg_ln_g[l], sg_ln_b[l], sg_w[l], sg_b[l])
        mixed = jnp.concatenate([y_a, y_b, y_c], axis=-1)
        g = group_norm_g[l].reshape(N_GROUPS, GROUP_DIM)
        mixed = rms_norm(mixed.reshape(b, s, N_GROUPS, GROUP_DIM), g).reshape(b, s, MIX_WIDTH)
        x = x + jnp.einsum('bse,ed->bsd', mixed, w_out[l])
        hdn = jnp.einsum('bsd,df->bsf', rms_norm(x, mlp_norm_g[l]), w_up[l])
        hdn = jnp.square(jax.nn.relu(hdn))
        x = x + jnp.einsum('bsf,fd->bsd', hdn, w_down[l])
    return rms_norm(x, final_norm_g)
```

```python
from contextlib import ExitStack
import numpy as np
import concourse.bass as bass
import concourse.mybir as mybir
from concourse.bass_utils import run_bass_kernel_spmd

F32 = mybir.dt.float32
BF16 = mybir.dt.bfloat16
AF = mybir.ActivationFunctionType
ALU = mybir.AluOpType

D = 1024
TT = 512
IN_W = 2816
DFF = 4096
EPS = 1e-6
NSLOT = 4
NPIECE = 24
PIPELINE = True


class Tok:
    __slots__ = ("name", "writer", "readers")

    def __init__(self, name=""):
        self.name = name
        self.writer = None
        self.readers = []


class Chan:
    __slots__ = ("sem", "count", "name")

    def __init__(self, sem, name=""):
        self.sem = sem
        self.count = 0
        self.name = name


class Ins:
    __slots__ = ("eng", "fn", "deps", "need_inc", "val", "chan")

    def __init__(self, eng, fn, chan=None):
        self.eng = eng
        self.fn = fn
        self.deps = []
        self.need_inc = False
        self.val = 0
        self.chan = chan


class Prog:
    ENGS = ("pe", "act", "dve", "pool", "sp")

    def __init__(self, nc):
        self.nc = nc
        self.ins = []
        self.h = {"pe": nc.tensor, "act": nc.scalar, "dve": nc.vector,
                  "pool": nc.gpsimd, "sp": nc.sync}

    def add(self, eng, fn, reads=(), writes=(), chan=None, extra=()):
        i = Ins(eng, fn, chan)
        deps = []
        is_dma = chan is not None

        def consider(p, kind):
            if p is None or p is i:
                return
            p_dma = p.chan is not None
            if not is_dma and not p_dma and p.eng == eng:
                if eng == "pe":
                    return
            if p not in deps:
                deps.append(p)

        for t in reads:
            consider(t.writer, "raw")
        for t in writes:
            consider(t.writer, "waw")
            for r in t.readers:
                consider(r, "war")
        for p in extra:
            if p not in deps:
                deps.append(p)
        for t in writes:
            t.writer = i
            t.readers = []
        for t in reads:
            if t.writer is not i:
                if not is_dma:
                    t.readers = [r for r in t.readers if r.chan is not None or r.eng != eng]
                t.readers.append(i)
        for d in deps:
            d.need_inc = True
        i.deps = deps
        self.ins.append(i)
        return i

    def emit(self, sems):
        cnt = {e: 0 for e in self.ENGS}
        for i in self.ins:
            if i.chan is not None:
                i.chan.count += 16
                i.val = i.chan.count
            elif i.need_inc:
                cnt[i.eng] += 1
                i.val = cnt[i.eng]
        seen = {e: {} for e in self.ENGS}
        for i in self.ins:
            h = self.h[i.eng]
            sd = seen[i.eng]
            for d in i.deps:
                if d.chan is not None:
                    key = id(d.chan)
                    sem = d.chan.sem
                else:
                    key = d.eng
                    sem = sems[d.eng]
                if sd.get(key, 0) < d.val:
                    h.wait_ge(sem, d.val)
                    sd[key] = d.val
            r = i.fn()
            if i.chan is not None:
                r.then_inc(i.chan.sem, 16)
            elif i.need_inc:
                r.then_inc(sems[i.eng], 1)
        return cnt


def piece_table():
    pcs = []
    pcs.append(("w_in", 0, 512, 8))
    pcs.append(("w_in", 512, 512, 8))
    pcs.append(("w_in", 1536, 512, 8))
    pcs.append(("w_in", 2048, 512, 8))
    pcs.append(("w_in", 1024, 512, 8))
    pcs.append(("w_in", 2560, 256, 8))
    pcs.append(("w_out", 0, 512, 8))
    pcs.append(("w_out", 512, 512, 8))
    for i in range(8):
        pcs.append(("w_up", 512 * i, 512, 8))
    for i in range(8):
        pcs.append(("w_down", 128 * i, 128, 32))
    return pcs


def build(depth, n_out_tiles):
    W = n_out_tiles + depth
    NW = W * TT
    NB = NW // 128
    nc = bass.Bass("TRN2", target_bir_lowering=False)

    def din(name, shape, dt=F32):
        return nc.dram_tensor(name, shape, dt, kind="ExternalInput").ap()

    xT_d = din("xT", [D, NW])
    valid_d = din("valid", [128, NB])
    wsrc = {"w_in": din("w_in", [depth, D, IN_W]), "w_out": din("w_out", [depth, D, D]),
            "w_up": din("w_up", [depth, D, DFF]), "w_down": din("w_down", [depth, DFF, D])}
    gains_d = din("gains", [128, (3 * depth + 1) * 8])
    convw_d = din("convw", [128, depth * 6])
    lng_d = din("lng", [128, depth * 256])
    lnb_d = din("lnb", [128, depth * 256])
    sgbb_d = din("sgbb", [128, depth * 2 * 128])
    sgwT_d = din("sgwT", [128, depth * 4 * 128])
    consts_d = din("consts", [128, 128 * 4 + 640])
    btab_d = din("btab", [depth, 8, 128, 640])
    outT_d = nc.dram_tensor("outT", [D, n_out_tiles * TT], F32, kind="ExternalOutput").ap()
    xs_d = nc.dram_tensor("xs", [D, NW], F32, kind="Internal").ap()
    wq_d = nc.dram_tensor("wq", [depth, NPIECE, 128, 4096], BF16, kind="Internal").ap()

    pcs = piece_table()

    with ExitStack() as es:
        def sb(name, shape, dt):
            return es.enter_context(nc.sbuf_tensor("sb_" + name, shape, dt))

        def new_sem(name):
            return es.enter_context(nc.semaphore(name))

        P = Prog(nc)
        sems = {e: new_sem("s_" + e) for e in Prog.ENGS}

        def chan(name):
            return Chan(new_sem("c_" + name), name)

        xt = [sb("xt%d" % i, [128, 8, TT], F32) for i in range(2)]
        xt_tok = [[Tok("xt%d_%d" % (i, c)) for c in range(8)] for i in range(2)]
        nT = sb("nT", [128, 8, TT], BF16)
        nT_tok = [Tok("nT%d" % c) for c in range(8)]
        nT2 = sb("nT2", [128, 8, TT], BF16)
        nT2_tok = [Tok("nT2_%d" % c) for c in range(8)]
        sq = [sb("sq%d" % i, [128, TT], BF16) for i in range(2)]
        sq_tok = [Tok("sq%d" % i) for i in range(2)]
        rstd = sb("rstd", [128, TT], F32)
        rstd_tok = Tok("rstd")
        qT = sb("qT", [128, 4, TT], BF16)
        qT_tok = [Tok("qT%d" % c) for c in range(4)]
        kT = sb("kT", [128, 4, 2 * TT], BF16)
        kT_tok = [[Tok("kT%d_%d" % (c, h)) for h in range(2)] for c in range(4)]
        vtm = sb("vtm", [128, 8, TT], BF16)
        vtm_tok = [Tok("vtm%d" % i) for i in range(8)]
        vbc = sb("vbc", [128, 8, 4], BF16)
        vbc_tok = [Tok("vbc%d" % i) for i in range(8)]
        bg32 = sb("bg32", [128, 2, TT], F32)
        bg_tok = [Tok("bg%d" % i) for i in range(2)]
        cg32 = sb("cg32", [128, 2, TT], F32)
        cg_tok = [Tok("cg%d" % i) for i in range(2)]
        zb = sb("zb", [128, 2, TT + 2], F32)
        zb_tok = [Tok("zb%d" % i) for i in range(2)]
        su32 = sb("su32", [128, 2, TT], F32)
        su_tok = [Tok("su%d" % i) for i in range(2)]
        svg = sb("svg", [128, 4, 256], F32)
        svg_tok = [Tok("svg%d" % i) for i in range(4)]
        vln = sb("vln", [128, 4, 256], BF16)
        vln_tok = [Tok("vln%d" % i) for i in range(4)]
        lnst = sb("lnst", [128, 4, 16], F32)
        lnst_tok = [Tok("lnst%d" % i) for i in range(4)]
        y32 = [sb("y32_%d" % i, [128, TT], F32) for i in range(2)]
        y32_tok = [Tok("y32_%d" % i) for i in range(2)]
        t32 = [sb("t32_0", [128, TT], F32)] * 2
        t32_tok = [Tok("t32_0")] * 2
        d2r = sb("d2r", [128, TT], BF16)
        d2_tok = Tok("d2r")
        sqE = [sb("sqE%d" % i, [128, TT], BF16) for i in range(2)]
        sqE_tok = [Tok("sqE%d" % i) for i in range(2)]
        NPT = 4
        PT = [sb("PT%d" % i, [128, TT], BF16) for i in range(NPT)]
        PT_tok = [Tok("PT%d" % i) for i in range(NPT)]
        mixT = sb("mixT", [128, 8, TT], BF16)
        mix_tok = [Tok("mix%d" % c) for c in range(8)]
        hdnT = sb("hdnT", [128, 32, TT], BF16)
        hdn_tok = [Tok("hdn%d" % c) for c in range(32)]
        r32 = [sb("r32_%d" % i, [128, TT], F32) for i in range(2)]
        r32_tok = [Tok("r32_%d" % i) for i in range(2)]
        wslot = [sb("wslot%d" % i, [128, 4096], BF16) for i in range(NSLOT)]
        wslot_tok = [Tok("wslot%d" % i) for i in range(NSLOT)]
        wslot_ch = [chan("ws%d" % i) for i in range(NSLOT)]
        btab = sb("btab", [128, 8, 640], BF16)
        btab_tok = [Tok("btab%d" % h) for h in range(8)]
        bstage = sb("bstage", [128, 640], F32)
        bstage_tok = Tok("bstage")
        bstage_ch = chan("bstage")
        wsT = sb("wsT", [128, 4, 128], BF16)
        wsT_tok = Tok("wsT")
        wstage = bstage
        wstage_tok = bstage_tok
        wstage_ch = bstage_ch
        lng = sb("lng", [128, 256], F32)
        lnb = sb("lnb", [128, 256], F32)
        sgbb = sb("sgbb", [128, 2, 128], F32)
        ltab_tok = Tok("ltab")
        ltab_ch = chan("ltab")
        gains = sb("gains", [128, (3 * depth + 1) * 8], F32)
        convw = sb("convw", [128, depth * 6], F32)
        valid_sb = sb("valid_sb", [128, NB], F32)
        consts = sb("consts", [128, 128 * 4 + 640], F32)
        ident_bf = sb("ident_bf", [128, 128], BF16)
        bdiag_bf = sb("bdiag_bf", [128, 128], BF16)
        ones_bf = sb("ones_bf", [128, 128], BF16)
        one64_bf = sb("one64_bf", [128, 64], BF16)
        sel_bf = sb("sel_bf", [128, 128], BF16)
        pat = sb("pat", [128, 4], BF16)
        par_tok = Tok("params")
        par_ch = chan("params")
        cst_tok = Tok("cst_bf")
        xld_ch = [chan("xld%d" % i) for i in range(2)]
        xst_ch = [chan("xst%d" % i) for i in range(2)]
        xs_tok = [Tok("xs%d" % i) for i in range(W)]
        wcv_ch = [[chan("wcv%d_%d" % (l, g)) for g in range(12)] for l in range(depth)]

        psum = [es.enter_context(nc.psum_tensor("ps%d" % i, [128, TT], F32)) for i in range(8)]
        ps_tok = [Tok("ps%d" % i) for i in range(8)]
        rr = {"A": 0, "B": 0, "AB": 0, "G": 0}
        dry = [False]

        den_tok4 = [Tok("denq%d" % i) for i in range(2)]

        def ps_alloc(pool="AB"):
            if pool == "A":
                i = rr["A"] % 3
            elif pool == "B":
                i = 3 + rr["B"] % 2
            elif pool == "G":
                i = 6 + rr["G"] % 2
            else:
                i = rr["AB"] % 6
            rr[pool] += 1
            return psum[i], ps_tok[i]

        V = nc.vector
        S = nc.scalar
        G = nc.gpsimd
        PE = nc.tensor

        def A(eng, reads, writes, f, *a, **k):
            if dry[0]:
                return None
            return P.add(eng, lambda: f(*a, **k), reads, writes)

        def DMA(q, ch, reads, writes, out, in_, extra=()):
            if dry[0]:
                return None
            h = {"sp": nc.sync, "act": nc.scalar, "pool": nc.gpsimd}[q]
            return P.add(q, lambda: h.dma_start(out=out, in_=in_), reads, writes, chan=ch, extra=extra)

        def MM(out, lhsT, rhs, start, stop, reads, writes, skip=False):
            if dry[0]:
                return None
            return P.add("pe", lambda: PE.matmul(out, lhsT, rhs, start=start, stop=stop, skip_group_check=skip),
                         reads, writes)

        def rsqrt_act(out, in_, bias, reads, out_tok, scale=1.0):
            A("act", reads, [out_tok], S.activation, out=out, in_=in_, func=AF.Ln, bias=bias, scale=scale)
            A("act", [out_tok], [out_tok], S.activation, out=out, in_=out, func=AF.Exp, scale=-0.5)

        last = None
        for (dst, src) in ((gains[:], gains_d), (convw[:], convw_d), (valid_sb[:], valid_d), (consts[:], consts_d)):
            last = DMA("sp", par_ch, [], [], dst, src)
        par_tok.writer = last
        A("dve", [par_tok], [cst_tok], V.tensor_copy, out=ident_bf[:], in_=consts[:, 0:128])
        A("dve", [par_tok], [cst_tok], V.tensor_copy, out=bdiag_bf[:], in_=consts[:, 128:256])
        A("dve", [par_tok], [cst_tok], V.tensor_copy, out=sel_bf[:], in_=consts[:, 1024:1152])
        A("pool", [], [cst_tok], G.memset, pat[:], 0.0)
        A("pool", [], [cst_tok], G.memset, pat[:, 0:1], 1.0)
        A("pool", [], [cst_tok], G.memset, pat[:, 2:3], 1.0)
        A("pool", [], [cst_tok], G.memset, ones_bf[:], 1.0 / 1024.0)
        A("pool", [], [cst_tok], G.memset, one64_bf[:], 1.0)
        for c in range(2):
            A("pool", [], [zb_tok[c]], G.memset, zb[:, c, :], 0.0)

        wconv_last = [[None] * 12 for _ in range(depth)]

        def grp_of(pi):
            return pi // 2

        def emit_wconv(l, plist=None):
            for pi, (wn, c0, ncol, kch) in enumerate(pcs):
                if plist is not None and pi not in plist:
                    continue
                src = wsrc[wn][l, :, c0:c0 + ncol].rearrange("(kc p) n -> p kc n", p=128)
                dst = wq_d[l, pi, :, 0:kch * ncol].rearrange("p (kc n) -> p kc n", kc=kch)
                g = grp_of(pi)
                wconv_last[l][g] = DMA("pool", wcv_ch[l][g], [], [], dst, src)

        emit_wconv(0, list(range(8)))

        slot_rr = [0, 0]

        def load_piece(l, pi):
            st = 0 if pi < 8 else 1
            s = 2 * st + slot_rr[st] % 2
            slot_rr[st] += 1
            wn, c0, ncol, kch = pcs[pi]
            n = kch * ncol
            DMA("sp", wslot_ch[s], [], [wslot_tok[s]], wslot[s][:, 0:n], wq_d[l, pi, :, 0:n],
                extra=[wconv_last[l][grp_of(pi)]])
            return wslot[s][:, 0:n].rearrange("p (kc n) -> p kc n", kc=kch), wslot_tok[s]

        def gcol(kind, l, c):
            base = (kind * depth + l) * 8 if kind < 3 else 3 * depth * 8
            return gains[:, base + c: base + c + 1]

        def layer_tables(l):
            for h in range(8):
                DMA("sp", bstage_ch, [], [bstage_tok], bstage[:], btab_d[l, h])
                A("dve", [bstage_tok, par_tok], [btab_tok[h]], V.tensor_tensor, out=btab[:, h, :], in0=bstage[:],
                  in1=consts[:, 384:1024], op=ALU.add)
            DMA("sp", wstage_ch, [], [wstage_tok], wstage[:, 0:512], sgwT_d[:, l * 512:(l + 1) * 512])
            for g in range(4):
                A("dve", [wstage_tok, par_tok], [wsT_tok], V.tensor_tensor, out=wsT[:, g, :],
                  in0=wstage[:, g * 128:(g + 1) * 128], in1=consts[:, 256:384], op=ALU.mult)
            last = None
            for (dst, src) in ((lng[:], lng_d[:, l * 256:(l + 1) * 256]), (lnb[:], lnb_d[:, l * 256:(l + 1) * 256]),
                               (sgbb[:], sgbb_d[:, l * 256:(l + 1) * 256].rearrange("p (c t) -> p c t", c=2))):
                last = DMA("sp", ltab_ch, [], [ltab_tok], dst, src)

        def rmsnorm(xb, kind, l, out_f32, pool="AB", dst=None, dst_tok=None, scratch=None):
            xtile = xt[xb]
            ps, pst = ps_alloc(pool)
            if scratch is None:
                sqb, sqt, rs, rs_tok = [sq[0][:], sq[1][:]], sq_tok, rstd[:], rstd_tok
            else:
                sqb, sqt, rs, rs_tok = scratch

            def sqr(c):
                s = c % 2
                if c % 2 == 0:
                    A("pool", [xt_tok[xb][c]], [sqt[s]], G.tensor_tensor, out=sqb[s], in0=xtile[:, c, :],
                      in1=xtile[:, c, :], op=ALU.mult)
                else:
                    A("dve", [xt_tok[xb][c]], [sqt[s]], V.tensor_tensor, out=sqb[s], in0=xtile[:, c, :],
                      in1=xtile[:, c, :], op=ALU.mult)
            sqr(0)
            for c in range(8):
                if c + 1 < 8:
                    sqr(c + 1)
                MM(ps[:], ones_bf[:], sqb[c % 2], c == 0, c == 7, [sqt[c % 2], cst_tok], [pst])
                yield 700.0
            rsqrt_act(rs, ps[:], EPS, [pst], rs_tok)
            yield 4000.0
            for c in range(8):
                if c == 4:
                    yield 3000.0
                if out_f32:
                    A("dve", [xt_tok[xb][c], rs_tok, par_tok], [xt_tok[xb][c]], V.scalar_tensor_tensor,
                      out=xtile[:, c, :], in0=xtile[:, c, :], scalar=gcol(kind, l, c), in1=rs,
                      op0=ALU.mult, op1=ALU.mult)
                else:
                    A("dve", [xt_tok[xb][c], rs_tok, par_tok], [dst_tok[c]], V.scalar_tensor_tensor,
                      out=dst[:, c, :], in0=xtile[:, c, :], scalar=gcol(kind, l, c), in1=rs,
                      op0=ALU.mult, op1=ALU.mult)

        gn_rr = [0]

        def gnorm(src, src_toks, den, den_tok, l, mc, pool):
            i = gn_rr[0] % 2
            gn_rr[0] += 1
            A("act", src_toks, [sqE_tok[i]], S.activation, out=sqE[i][:], in_=src, func=AF.Square)
            ps, pst = ps_alloc(pool)
            if den is not None:
                dbank, pb = den
                A("act", [den_tok], [d2_tok], S.activation, out=d2r[pb:pb + 2, :], in_=dbank[pb:pb + 2, :],
                  func=AF.Square, scale=1e-3)
                MM(ps[:], bdiag_bf[:], sqE[i][:], True, False, [sqE_tok[i], cst_tok], [pst])
                for hh in range(2):
                    MM(ps[hh * 64:(hh + 1) * 64, :], sel_bf[pb:pb + 2, hh * 64:(hh + 1) * 64], d2r[pb:pb + 2, :],
                       False, True, [d2_tok, cst_tok], [pst])
                rsqrt_act(t32[i][:], ps[:], 1e-18, [pst], t32_tok[i])
            else:
                MM(ps[:], bdiag_bf[:], sqE[i][:], True, True, [sqE_tok[i], cst_tok], [pst])
                rsqrt_act(t32[i][:], ps[:], EPS, [pst], t32_tok[i])
            A("dve", src_toks + [t32_tok[i], par_tok], [mix_tok[mc]], V.scalar_tensor_tensor, out=mixT[:, mc, :],
              in0=src, scalar=gcol(1, l, mc), in1=t32[i][:], op0=ALU.mult, op1=ALU.mult)

        pt_rr = [0]
        y_rr = [0]
        r_rr = [0]

        def mmcost(n, cols):
            return n * max(64.0, cols * 0.52)

        def front(l, ti, xb, full):
            half = ti % 2
            src = xT_d if l == 0 else xs_d
            rd = [] if l == 0 else [xs_tok[ti]]
            DMA("sp", xld_ch[xb], rd, xt_tok[xb], xt[xb][:],
                src[:, ti * TT:(ti + 1) * TT].rearrange("(c p) t -> p c t", p=128))
            yield from rmsnorm(xb, 0, l, False, "AB", nT, nT_tok)

            def fm_group(slot, stok, oc):
                ps, pst = ps_alloc("AB")
                for kc in range(8):
                    MM(ps[:], slot[:, kc, oc * 128:(oc + 1) * 128], nT[:, kc, :], kc == 0, kc == 7,
                       [stok, nT_tok[kc]], [pst])
                return ps, pst

            if full:
                slot, stok = load_piece(l, 0)
                for oc in range(4):
                    ps, pst = fm_group(slot, stok, oc)
                    A("act", [pst], [qT_tok[oc]], S.mul, out=qT[:, oc, :], in_=ps[:], mul=0.125)
                    yield mmcost(8, 512)
            slot, stok = load_piece(l, 1)
            for oc in range(4):
                ps, pst = fm_group(slot, stok, oc)
                A("act", [pst], [kT_tok[oc][half]], S.copy, out=kT[:, oc, half * TT:(half + 1) * TT], in_=ps[:])
                yield mmcost(8, 512)
            slot, stok = load_piece(l, 2)
            for oc in range(4):
                if oc < 2 and not full:
                    continue
                ps, pst = fm_group(slot, stok, oc)
                if oc < 2:
                    A("act", [pst], [bg_tok[oc]], S.copy, out=bg32[:, oc, :], in_=ps[:])
                else:
                    A("act", [pst], [cg_tok[oc - 2]], S.copy, out=cg32[:, oc - 2, :], in_=ps[:])
                yield mmcost(8, 512)
            slot, stok = load_piece(l, 3)
            for oc in range(4):
                if oc >= 2 and not full:
                    continue
                ps, pst = fm_group(slot, stok, oc)
                if oc < 2:
                    A("dve", [pst, cg_tok[oc]], [zb_tok[oc]], V.tensor_tensor, out=zb[:, oc, 2:TT + 2], in0=ps[:],
                      in1=cg32[:, oc, :], op=ALU.mult)
                else:
                    A("act", [pst], [su_tok[oc - 2]], S.activation, out=su32[:, oc - 2, :], in_=ps[:], func=AF.Gelu)
                yield mmcost(8, 512)
            slot, stok = load_piece(l, 4)
            for blk in range(4):
                ps, pst = ps_alloc("AB")
                for kc in range(8):
                    MM(ps[:], nT[:, kc, blk * 128:(blk + 1) * 128], slot[:, kc, :], kc == 0, kc == 7,
                       [stok, nT_tok[kc]], [pst])
                vb = half * 4 + blk
                A("act", [pst], [vtm_tok[vb]], S.copy, out=vtm[:, vb, :], in_=ps[:])
                A("pool", [par_tok, cst_tok], [vbc_tok[vb]], G.tensor_scalar, out=vbc[:, vb, :], in0=pat[:],
                  scalar1=valid_sb[:, ti * 4 + blk: ti * 4 + blk + 1], scalar2=None, op0=ALU.mult)
                yield mmcost(8, 512)
            if not full:
                for c in range(2):
                    A("dve", [zb_tok[c]], [zb_tok[c]], V.tensor_copy, out=zb[:, c, 0:2], in_=zb[:, c, TT:TT + 2])
                return
            slot, stok = load_piece(l, 5)
            for blk in range(4):
                ps, pst = ps_alloc("AB")
                for kc in range(8):
                    MM(ps[:, 0:256], nT[:, kc, blk * 128:(blk + 1) * 128], slot[:, kc, :], kc == 0, kc == 7,
                       [stok, nT_tok[kc]], [pst])
                A("act", [pst], [svg_tok[blk]], S.activation, out=svg[:, blk, :], in_=ps[:, 0:256], func=AF.Gelu)
                st = lnst[:, blk, :]
                lt = lnst_tok[blk]
                A("dve", [svg_tok[blk]], [lt], V.bn_stats, out=st[:, 0:6], in_=svg[:, blk, :])
                A("dve", [lt], [lt], V.bn_aggr, out=st[:, 6:8], in_=st[:, 0:6])
                A("act", [lt], [lt], S.activation, out=st[:, 8:9], in_=st[:, 7:8], func=AF.Ln, bias=EPS, scale=1.0)
                A("act", [lt], [lt], S.activation, out=st[:, 9:10], in_=st[:, 8:9], func=AF.Exp, scale=-0.5)
                A("dve", [lt, svg_tok[blk]], [svg_tok[blk]], V.tensor_scalar, out=svg[:, blk, :], in0=svg[:, blk, :],
                  scalar1=st[:, 6:7], scalar2=st[:, 9:10], op0=ALU.subtract, op1=ALU.mult)
                A("pool", [svg_tok[blk], ltab_tok], [svg_tok[blk]], G.tensor_tensor, out=svg[:, blk, :],
                  in0=svg[:, blk, :], in1=lng[:], op=ALU.mult)
                A("pool", [svg_tok[blk], ltab_tok], [vln_tok[blk]], G.tensor_tensor, out=vln[:, blk, :],
                  in0=svg[:, blk, :], in1=lnb[:], op=ALU.add)
                yield mmcost(8, 256) + 2000.0

            steps = [(c, hh, j) for c in range(4) for hh in range(2) for j in range(8)]
            accs = {}

            def geom(j):
                b_lo, b_hi = max(0, j - 4), min(3, j)
                nq = 128 * (b_hi - b_lo + 1)
                kh = (half + 1) % 2 if j < 4 else half
                return b_lo, nq, b_lo - j + 4, kh, j % 4

            def emit_S(k):
                c, hh, j = steps[k]
                h = 2 * c + hh
                p0, p1 = hh * 64, (hh + 1) * 64
                b_lo, nq, e_lo, kh, kb = geom(j)
                sps, spt = ps_alloc("A")
                MM(sps[:, 0:nq], kT[p0:p1, c, kh * TT + kb * 128: kh * TT + (kb + 1) * 128],
                   qT[p0:p1, c, b_lo * 128: b_lo * 128 + nq], True, False,
                   [kT_tok[c][kh], qT_tok[c]], [spt])
                MM(sps[:, 0:nq], ident_bf[:], btab[:, h, e_lo * 128: e_lo * 128 + nq], False, True,
                   [cst_tok, btab_tok[h]], [spt])
                pi = pt_rr[0] % NPT
                pt_rr[0] += 1
                A("act", [spt], [PT_tok[pi]], S.activation, out=PT[pi][:, 0:nq], in_=sps[:, 0:nq], func=AF.Exp)
                return pi

            def emit_PV(k, pi):
                c, hh, j = steps[k]
                h = 2 * c + hh
                p0, p1 = hh * 64, (hh + 1) * 64
                b_lo, nq, e_lo, kh, kb = geom(j)
                if (hh, j) == (0, 0):
                    accs[c] = ps_alloc("B")
                num, num_t = accs[c]
                den, den_t, pb = psum[5], den_tok4[c % 2], 32 * (c % 2)
                vb = kh * 4 + kb
                q0 = b_lo * 128
                MM(num[p0:p1, q0:q0 + nq], vtm[:, vb, h * 64:(h + 1) * 64], PT[pi][:, 0:nq],
                   j == 0, j == 7, [vtm_tok[vb], PT_tok[pi]], [num_t], skip=True)
                MM(den[pb:pb + 2, q0:q0 + nq], vbc[:, vb, hh:hh + 2], PT[pi][:, 0:nq],
                   (hh, j) == (0, 0), (hh, j) == (1, 7), [vbc_tok[vb], PT_tok[pi]], [den_t], skip=True)
                if (hh, j) == (1, 7):
                    gnorm(num[:], [num_t], (den, pb), den_t, l, c, "A")
                    return 5000.0
                return 0.0

            DEPTH_SW = 2
            pis = [emit_S(k) for k in range(DEPTH_SW)]
            for k in range(len(steps)):
                if k + DEPTH_SW < len(steps):
                    pis.append(emit_S(k + DEPTH_SW))
                extra_c = emit_PV(k, pis[k])
                yield mmcost(4, geom(steps[k][2])[1]) + extra_c

            for c in range(2):
                i = y_rr[0] % 2
                y_rr[0] += 1
                cw = lambda j: convw[:, (l * 3 + j) * 2 + c: (l * 3 + j) * 2 + c + 1]
                A("pool", [zb_tok[c], par_tok], [y32_tok[i]], G.tensor_scalar, out=y32[i][:], in0=zb[:, c, 2:TT + 2],
                  scalar1=cw(2), scalar2=None, op0=ALU.mult)
                A("dve", [zb_tok[c], y32_tok[i], par_tok], [y32_tok[i]], V.scalar_tensor_tensor, out=y32[i][:],
                  in0=zb[:, c, 1:TT + 1], scalar=cw(1), in1=y32[i][:], op0=ALU.mult, op1=ALU.add)
                A("dve", [zb_tok[c], y32_tok[i], par_tok], [y32_tok[i]], V.scalar_tensor_tensor, out=y32[i][:],
                  in0=zb[:, c, 0:TT], scalar=cw(0), in1=y32[i][:], op0=ALU.mult, op1=ALU.add)
                A("pool", [bg_tok[c], y32_tok[i]], [y32_tok[i]], G.tensor_tensor, out=y32[i][:], in0=y32[i][:],
                  in1=bg32[:, c, :], op=ALU.mult)
                A("dve", [zb_tok[c]], [zb_tok[c]], V.tensor_copy, out=zb[:, c, 0:2], in_=zb[:, c, TT:TT + 2])
                gnorm(y32[i][:], [y32_tok[i]], None, None, l, 4 + c, "AB")
                yield 8000.0

            for c2 in range(2):
                ps, pst = ps_alloc("AB")
                for blk in range(4):
                    for gg in range(2):
                        g = 2 * c2 + gg
                        MM(ps[gg * 64:(gg + 1) * 64, blk * 128:(blk + 1) * 128], vln[:, blk, g * 64:(g + 1) * 64],
                           wsT[:, g, :], True, True, [vln_tok[blk], wsT_tok], [pst])
                i = y_rr[0] % 2
                y_rr[0] += 1
                for blk in range(4):
                    A("dve", [pst, ltab_tok], [y32_tok[i]], V.tensor_tensor, out=y32[i][:, blk * 128:(blk + 1) * 128],
                      in0=ps[:, blk * 128:(blk + 1) * 128], in1=sgbb[:, c2, :], op=ALU.add)
                A("pool", [su_tok[c2], y32_tok[i]], [y32_tok[i]], G.tensor_tensor, out=y32[i][:], in0=y32[i][:],
                  in1=su32[:, c2, :], op=ALU.mult)
                gnorm(y32[i][:], [y32_tok[i]], None, None, l, 6 + c2, "AB")
                yield 8000.0

            for p2 in range(2):
                slot, stok = load_piece(l, 6 + p2)
                for oc in range(4):
                    ps, pst = ps_alloc("AB")
                    for kc in range(8):
                        MM(ps[:], slot[:, kc, oc * 128:(oc + 1) * 128], mixT[:, kc, :], kc == 0, kc == 7,
                           [stok, mix_tok[kc]], [pst])
                    xc = p2 * 4 + oc
                    A("dve", [pst, xt_tok[xb][xc]], [xt_tok[xb][xc]], V.tensor_tensor, out=xt[xb][:, xc, :],
                      in0=ps[:], in1=xt[xb][:, xc, :], op=ALU.add)
                    yield mmcost(8, 512)
            yield from rmsnorm(xb, 2, l, False, "AB", nT2, nT2_tok)

        def ffn(l, ti, xb):
            last_layer = (l == depth - 1)
            for p8 in range(8):
                slot, stok = load_piece(l, 8 + p8)
                for oc in range(4):
                    ps, pst = ps_alloc("G")
                    for kc in range(8):
                        MM(ps[:], slot[:, kc, oc * 128:(oc + 1) * 128], nT2[:, kc, :], kc == 0, kc == 7,
                           [stok, nT2_tok[kc]], [pst])
                    i = r_rr[0] % 2
                    r_rr[0] += 1
                    fc = p8 * 4 + oc
                    A("dve", [pst], [r32_tok[i]], V.tensor_scalar, out=r32[i][:], in0=ps[:], scalar1=0.0, scalar2=None,
                      op0=ALU.max)
                    A("dve", [r32_tok[i]], [hdn_tok[fc]], V.tensor_tensor, out=hdnT[:, fc, :], in0=r32[i][:],
                      in1=r32[i][:], op=ALU.mult)
                    yield mmcost(8, 512)
            for oc in range(8):
                slot, stok = load_piece(l, 16 + oc)
                ps, pst = ps_alloc("G")
                for kc in range(32):
                    MM(ps[:], slot[:, kc, :], hdnT[:, kc, :], kc == 0, kc == 31, [stok, hdn_tok[kc]], [pst])
                    if kc % 8 == 7 and kc != 31:
                        yield mmcost(8, 512)
                A("dve", [pst, xt_tok[xb][oc]], [xt_tok[xb][oc]], V.tensor_tensor, out=xt[xb][:, oc, :],
                  in0=ps[:], in1=xt[xb][:, oc, :], op=ALU.add)
                yield mmcost(8, 512)
            if last_layer:
                yield from rmsnorm(xb, 3, l, True, "G", scratch=([hdnT[:, 0, :], hdnT[:, 1, :]], [hdn_tok[0], hdn_tok[1]],
                                                               r32[0][:], r32_tok[0]))
                to = ti - depth
                DMA("sp", xst_ch[xb], xt_tok[xb], [], outT_d[:, to * TT:(to + 1) * TT].rearrange("(c p) t -> p c t", p=128),
                    xt[xb][:])
            else:
                DMA("sp", xst_ch[xb], xt_tok[xb], [xs_tok[ti]],
                    xs_d[:, ti * TT:(ti + 1) * TT].rearrange("(c p) t -> p c t", p=128), xt[xb][:])
            yield mmcost(8, 512)

        def total_cost(mk):
            save = (dict(rr), list(slot_rr), pt_rr[0], y_rr[0], r_rr[0], gn_rr[0])
            dry[0] = True
            tot = sum(mk())
            dry[0] = False
            rr.clear()
            rr.update(save[0])
            slot_rr[:] = save[1]
            pt_rr[0], y_rr[0], r_rr[0], gn_rr[0] = save[2:]
            return tot

        def merge(mka, mkb):
            if mkb is None:
                for _ in mka():
                    pass
                return
            if mka is None:
                for _ in mkb():
                    pass
                return
            ta, tb = total_cost(mka), total_cost(mkb)
            ga, gb = mka(), mkb()
            ca = cb = 0.0
            da = db = False
            while not (da and db):
                pick_a = (not da) and (db or ca / ta <= cb / tb)
                if pick_a:
                    try:
                        ca += next(ga)
                    except StopIteration:
                        da = True
                else:
                    try:
                        cb += next(gb)
                    except StopIteration:
                        db = True

        xb = 0
        pending = None
        for l in range(depth):
            for ti in range(l, W):
                if ti == l:
                    if PIPELINE and pending is not None:
                        pass
                    layer_tables(l)
                if l + 1 < depth:
                    per = -(-NPIECE // max(1, W - l - 1))
                    k0 = (ti - l) * per
                    emit_wconv(l + 1, list(range(k0, min(k0 + per, NPIECE))))
                full = ti > l
                mk_front = (lambda l=l, ti=ti, xb=xb, full=full: front(l, ti, xb, full))
                if PIPELINE:
                    merge(mk_front, pending)
                else:
                    merge(mk_front, None)
                    if pending is not None:
                        merge(pending, None)
                    pending = None
                if l == 0 and ti == 0:
                    emit_wconv(0, list(range(8, NPIECE)))
                pending = (lambda l=l, ti=ti, xb=xb: ffn(l, ti, xb)) if full else None
                if not PIPELINE and pending is not None:
                    merge(pending, None)
                    pending = None
                xb ^= 1
        if pending is not None:
            merge(pending, None)

        cnt = P.emit(sems)
        for c in xst_ch:
            if c.count:
                nc.sync.wait_ge(c.sem, c.count)
        print("[kernel] instrs=%d sem counts=%s" % (len(P.ins), cnt))
    return nc


def host_prep(inputs, depth, n_out_tiles, ncore_per_seq):
    x = np.asarray(inputs["x"], dtype=np.float32)
    B, SEQ, _ = x.shape
    TPC = n_out_tiles * TT
    assert TPC * ncore_per_seq == SEQ
    halo = depth * TT
    NW = TPC + halo
    NB = NW // 128
    f = lambda k: np.asarray(inputs[k], dtype=np.float32)

    def fm(v):
        v = v.reshape(-1, 8, 128)
        return np.ascontiguousarray(v.transpose(2, 0, 1)).reshape(128, -1)

    gains = np.concatenate([fm(f("mix_norm_g")), fm(f("group_norm_g")), fm(f("mlp_norm_g")),
                            fm(f("final_norm_g")[None])], axis=1)
    cw = f("conv_w").reshape(depth, 3, 2, 128)
    convw = np.ascontiguousarray(cw.transpose(3, 0, 1, 2)).reshape(128, depth * 6)
    lng = np.ascontiguousarray(np.broadcast_to(f("sg_ln_g").reshape(1, depth * 256), (128, depth * 256)))
    lnb = np.ascontiguousarray(np.broadcast_to(f("sg_ln_b").reshape(1, depth * 256), (128, depth * 256)))
    sb_ = f("sg_b")
    t1 = sb_.reshape(depth, 2, 2, 1, 128)
    t1 = np.broadcast_to(t1, (depth, 2, 2, 64, 128))
    sgbb = np.ascontiguousarray(t1.transpose(2, 3, 0, 1, 4)).reshape(128, depth * 2 * 128)
    sgwT = np.ascontiguousarray(f("sg_w").transpose(3, 0, 1, 2)).reshape(128, depth * 4 * 128)
    ident = np.eye(128, dtype=np.float32)
    bd = np.zeros((128, 128), np.float32)
    bd[:64, :64] = 1.0 / 64
    bd[64:, 64:] = 1.0 / 64
    triu = np.triu(np.ones((128, 128), np.float32))
    kk = np.arange(128)[:, None]
    cc = np.arange(640)[None, :]
    maskc = np.zeros((128, 640), np.float32)
    maskc[(kk >= 64) & (cc < 64)] = -30000.0
    maskc[(kk < 64) & (cc >= 576)] = -30000.0
    sel = np.zeros((128, 128), np.float32)
    sel[np.arange(128) % 32 == 0, 0:64] = 1.0
    sel[np.arange(128) % 32 == 1, 64:128] = 1.0
    consts = np.concatenate([ident, bd, triu, maskc, sel], axis=1)
    idx = np.clip(cc - kk, -128, 128) + 128
    btab = np.ascontiguousarray(f("rel_bias")[:, :, idx])
    shared = {"w_in": f("w_in"), "w_out": f("w_out"), "w_up": f("w_up"), "w_down": f("w_down"),
              "gains": gains, "convw": convw, "lng": lng, "lnb": lnb, "sgbb": sgbb, "sgwT": sgwT,
              "consts": consts, "btab": btab}
    in_maps = []
    for core in range(B * ncore_per_seq):
        b, q = divmod(core, ncore_per_seq)
        s0 = q * TPC - halo
        win = np.zeros((NW, D), np.float32)
        lo = max(s0, 0)
        win[lo - s0:] = x[b, lo:(q + 1) * TPC]
        val = np.zeros((NW,), np.float32)
        val[lo - s0:] = 1.0
        m = dict(shared)
        m["xT"] = np.ascontiguousarray(win.T)
        m["valid"] = np.ascontiguousarray(val.reshape(NB, 128).T)
        in_maps.append(m)
    return in_maps


def run(inputs, depth, n_out_tiles, ncore_per_seq):
    x = inputs["x"]
    B, SEQ, _ = x.shape
    in_maps = host_prep(inputs, depth, n_out_tiles, ncore_per_seq)
    nc = build(depth, n_out_tiles)
    n = len(in_maps)
    res = run_bass_kernel_spmd(nc, in_maps, core_ids=list(range(n)))
    TPC = n_out_tiles * TT
    out = np.empty((B, SEQ, D), np.float32)
    for core in range(n):
        b, q = divmod(core, ncore_per_seq)
        out[b, q * TPC:(q + 1) * TPC] = res.results[core]["outT"].T
    return out


def kernel(**inputs):
    return run(inputs, depth=4, n_out_tiles=8, ncore_per_seq=4)
```

```python
from contextlib import ExitStack
import numpy as np
import concourse.bass as bass
import concourse.mybir as mybir
from concourse.bass_utils import run_bass_kernel_spmd

F32 = mybir.dt.float32
BF16 = mybir.dt.bfloat16
AF = mybir.ActivationFunctionType
ALU = mybir.AluOpType

D = 1024
TT = 512
IN_W = 2816
DFF = 4096
EPS = 1e-6
NSLOT = 4
NPIECE = 24
PIPELINE = True


class Tok:
    __slots__ = ("name", "writer", "readers")

    def __init__(self, name=""):
        self.name = name
        self.writer = None
        self.readers = []


class Chan:
    __slots__ = ("sem", "count", "name")

    def __init__(self, sem, name=""):
        self.sem = sem
        self.count = 0
        self.name = name


class Ins:
    __slots__ = ("eng", "fn", "deps", "need_inc", "val", "chan")

    def __init__(self, eng, fn, chan=None):
        self.eng = eng
        self.fn = fn
        self.deps = []
        self.need_inc = False
        self.val = 0
        self.chan = chan


class Prog:
    ENGS = ("pe", "act", "dve", "pool", "sp")

    def __init__(self, nc):
        self.nc = nc
        self.ins = []
        self.h = {"pe": nc.tensor, "act": nc.scalar, "dve": nc.vector,
                  "pool": nc.gpsimd, "sp": nc.sync}

    def add(self, eng, fn, reads=(), writes=(), chan=None, extra=()):
        i = Ins(eng, fn, chan)
        deps = []
        is_dma = chan is not None

        def consider(p, kind):
            if p is None or p is i:
                return
            p_dma = p.chan is not None
            if not is_dma and not p_dma and p.eng == eng:
                if eng == "pe":
                    return
            if p not in deps:
                deps.append(p)

        for t in reads:
            consider(t.writer, "raw")
        for t in writes:
            consider(t.writer, "waw")
            for r in t.readers:
                consider(r, "war")
        for p in extra:
            if p not in deps:
                deps.append(p)
        for t in writes:
            t.writer = i
            t.readers = []
        for t in reads:
            if t.writer is not i:
                if not is_dma:
                    t.readers = [r for r in t.readers if r.chan is not None or r.eng != eng]
                t.readers.append(i)
        for d in deps:
            d.need_inc = True
        i.deps = deps
        self.ins.append(i)
        return i

    def emit(self, sems):
        cnt = {e: 0 for e in self.ENGS}
        for i in self.ins:
            if i.chan is not None:
                i.chan.count += 16
                i.val = i.chan.count
            elif i.need_inc:
                cnt[i.eng] += 1
                i.val = cnt[i.eng]
        seen = {e: {} for e in self.ENGS}
        for i in self.ins:
            h = self.h[i.eng]
            sd = seen[i.eng]
            for d in i.deps:
                if d.chan is not None:
                    key = id(d.chan)
                    sem = d.chan.sem
                else:
                    key = d.eng
                    sem = sems[d.eng]
                if sd.get(key, 0) < d.val:
                    h.wait_ge(sem, d.val)
                    sd[key] = d.val
            r = i.fn()
            if i.chan is not None:
                r.then_inc(i.chan.sem, 16)
            elif i.need_inc:
                r.then_inc(sems[i.eng], 1)
        return cnt


def piece_table():
    pcs = []
    pcs.append(("w_in", 0, 512, 8))
    pcs.append(("w_in", 512, 512, 8))
    pcs.append(("w_in", 1536, 512, 8))
    pcs.append(("w_in", 2048, 512, 8))
    pcs.append(("w_in", 1024, 512, 8))
    pcs.append(("w_in", 2560, 256, 8))
    pcs.append(("w_out", 0, 512, 8))
    pcs.append(("w_out", 512, 512, 8))
    for i in range(8):
        pcs.append(("w_up", 512 * i, 512, 8))
    for i in range(8):
        pcs.append(("w_down", 128 * i, 128, 32))
    return pcs


def build(depth, n_out_tiles):
    W = n_out_tiles + depth
    NW = W * TT
    NB = NW // 128
    nc = bass.Bass("TRN2", target_bir_lowering=False)

    def din(name, shape, dt=F32):
        return nc.dram_tensor(name, shape, dt, kind="ExternalInput").ap()

    xT_d = din("xT", [D, NW])
    valid_d = din("valid", [128, NB])
    wsrc = {"w_in": din("w_in", [depth, D, IN_W]), "w_out": din("w_out", [depth, D, D]),
            "w_up": din("w_up", [depth, D, DFF]), "w_down": din("w_down", [depth, DFF, D])}
    gains_d = din("gains", [128, (3 * depth + 1) * 8])
    convw_d = din("convw", [128, depth * 6])
    lng_d = din("lng", [128, depth * 256])
    lnb_d = din("lnb", [128, depth * 256])
    sgbb_d = din("sgbb", [128, depth * 2 * 128])
    sgwT_d = din("sgwT", [128, depth * 4 * 128])
    consts_d = din("consts", [128, 128 * 4 + 640])
    btab_d = din("btab", [depth, 8, 128, 640])
    outT_d = nc.dram_tensor("outT", [D, n_out_tiles * TT], F32, kind="ExternalOutput").ap()
    xs_d = nc.dram_tensor("xs", [D, NW], F32, kind="Internal").ap()
    wq_d = nc.dram_tensor("wq", [depth, NPIECE, 128, 4096], BF16, kind="Internal").ap()

    pcs = piece_table()

    with ExitStack() as es:
        def sb(name, shape, dt):
            return es.enter_context(nc.sbuf_tensor("sb_" + name, shape, dt))

        def new_sem(name):
            return es.enter_context(nc.semaphore(name))

        P = Prog(nc)
        sems = {e: new_sem("s_" + e) for e in Prog.ENGS}

        def chan(name):
            return Chan(new_sem("c_" + name), name)

        xt = [sb("xt%d" % i, [128, 8, TT], F32) for i in range(2)]
        xt_tok = [[Tok("xt%d_%d" % (i, c)) for c in range(8)] for i in range(2)]
        nT = sb("nT", [128, 8, TT], BF16)
        nT_tok = [Tok("nT%d" % c) for c in range(8)]
        nT2 = sb("nT2", [128, 8, TT], BF16)
        nT2_tok = [Tok("nT2_%d" % c) for c in range(8)]
        sq = [sb("sq%d" % i, [128, TT], BF16) for i in range(2)]
        sq_tok = [Tok("sq%d" % i) for i in range(2)]
        rstd = sb("rstd", [128, TT], F32)
        rstd_tok = Tok("rstd")
        qT = sb("qT", [128, 4, TT], BF16)
        qT_tok = [Tok("qT%d" % c) for c in range(4)]
        kT = sb("kT", [128, 4, 2 * TT], BF16)
        kT_tok = [[Tok("kT%d_%d" % (c, h)) for h in range(2)] for c in range(4)]
        vtm = sb("vtm", [128, 8, TT], BF16)
        vtm_tok = [Tok("vtm%d" % i) for i in range(8)]
        vbc = sb("vbc", [128, 8, 4], BF16)
        vbc_tok = [Tok("vbc%d" % i) for i in range(8)]
        bg32 = sb("bg32", [128, 2, TT], F32)
        bg_tok = [Tok("bg%d" % i) for i in range(2)]
        cg32 = sb("cg32", [128, 2, TT], F32)
        cg_tok = [Tok("cg%d" % i) for i in range(2)]
        zb = sb("zb", [128, 2, TT + 2], F32)
        zb_tok = [Tok("zb%d" % i) for i in range(2)]
        su32 = sb("su32", [128, 2, TT], F32)
        su_tok = [Tok("su%d" % i) for i in range(2)]
        svg = sb("svg", [128, 4, 256], F32)
        svg_tok = [Tok("svg%d" % i) for i in range(4)]
        vln = sb("vln", [128, 4, 256], BF16)
        vln_tok = [Tok("vln%d" % i) for i in range(4)]
        lnst = sb("lnst", [128, 4, 16], F32)
        lnst_tok = [Tok("lnst%d" % i) for i in range(4)]
        y32 = [sb("y32_%d" % i, [128, TT], F32) for i in range(2)]
        y32_tok = [Tok("y32_%d" % i) for i in range(2)]
        t32 = [sb("t32_0", [128, TT], F32)] * 2
        t32_tok = [Tok("t32_0")] * 2
        d2r = sb("d2r", [128, TT], BF16)
        d2_tok = Tok("d2r")
        sqE = [sb("sqE%d" % i, [128, TT], BF16) for i in range(2)]
        sqE_tok = [Tok("sqE%d" % i) for i in range(2)]
        NPT = 4
        PT = [sb("PT%d" % i, [128, TT], BF16) for i in range(NPT)]
        PT_tok = [Tok("PT%d" % i) for i in range(NPT)]
        mixT = sb("mixT", [128, 8, TT], BF16)
        mix_tok = [Tok("mix%d" % c) for c in range(8)]
        hdnT = sb("hdnT", [128, 32, TT], BF16)
        hdn_tok = [Tok("hdn%d" % c) for c in range(32)]
        r32 = [sb("r32_%d" % i, [128, TT], F32) for i in range(2)]
        r32_tok = [Tok("r32_%d" % i) for i in range(2)]
        wslot = [sb("wslot%d" % i, [128, 4096], BF16) for i in range(NSLOT)]
        wslot_tok = [Tok("wslot%d" % i) for i in range(NSLOT)]
        wslot_ch = [chan("ws%d" % i) for i in range(NSLOT)]
        btab = sb("btab", [128, 8, 640], BF16)
        btab_tok = [Tok("btab%d" % h) for h in range(8)]
        bstage = sb("bstage", [128, 640], F32)
        bstage_tok = Tok("bstage")
        bstage_ch = chan("bstage")
        wsT = sb("wsT", [128, 4, 128], BF16)
        wsT_tok = Tok("wsT")
        wstage = bstage
        wstage_tok = bstage_tok
        wstage_ch = bstage_ch
        lng = sb("lng", [128, 256], F32)
        lnb = sb("lnb", [128, 256], F32)
        sgbb = sb("sgbb", [128, 2, 128], F32)
        ltab_tok = Tok("ltab")
        ltab_ch = chan("ltab")
        gains = sb("gains", [128, (3 * depth + 1) * 8], F32)
        convw = sb("convw", [128, depth * 6], F32)
        valid_sb = sb("valid_sb", [128, NB], F32)
        consts = sb("consts", [128, 128 * 4 + 640], F32)
        ident_bf = sb("ident_bf", [128, 128], BF16)
        bdiag_bf = sb("bdiag_bf", [128, 128], BF16)
        ones_bf = sb("ones_bf", [128, 128], BF16)
        one64_bf = sb("one64_bf", [128, 64], BF16)
        sel_bf = sb("sel_bf", [128, 128], BF16)
        pat = sb("pat", [128, 4], BF16)
        par_tok = Tok("params")
        par_ch = chan("params")
        cst_tok = Tok("cst_bf")
        xld_ch = [chan("xld%d" % i) for i in range(2)]
        xst_ch = [chan("xst%d" % i) for i in range(2)]
        xs_tok = [Tok("xs%d" % i) for i in range(W)]
        wcv_ch = [[chan("wcv%d_%d" % (l, g)) for g in range(12)] for l in range(depth)]

        psum = [es.enter_context(nc.psum_tensor("ps%d" % i, [128, TT], F32)) for i in range(8)]
        ps_tok = [Tok("ps%d" % i) for i in range(8)]
        rr = {"A": 0, "B": 0, "AB": 0, "G": 0}
        dry = [False]

        den_tok4 = [Tok("denq%d" % i) for i in range(2)]

        def ps_alloc(pool="AB"):
            if pool == "A":
                i = rr["A"] % 3
            elif pool == "B":
                i = 3 + rr["B"] % 2
            elif pool == "G":
                i = 6 + rr["G"] % 2
            else:
                i = rr["AB"] % 6
            rr[pool] += 1
            return psum[i], ps_tok[i]

        V = nc.vector
        S = nc.scalar
        G = nc.gpsimd
        PE = nc.tensor

        def A(eng, reads, writes, f, *a, **k):
            if dry[0]:
                return None
            return P.add(eng, lambda: f(*a, **k), reads, writes)

        def DMA(q, ch, reads, writes, out, in_, extra=()):
            if dry[0]:
                return None
            h = {"sp": nc.sync, "act": nc.scalar, "pool": nc.gpsimd}[q]
            return P.add(q, lambda: h.dma_start(out=out, in_=in_), reads, writes, chan=ch, extra=extra)

        def MM(out, lhsT, rhs, start, stop, reads, writes, skip=False):
            if dry[0]:
                return None
            return P.add("pe", lambda: PE.matmul(out, lhsT, rhs, start=start, stop=stop, skip_group_check=skip),
                         reads, writes)

        def rsqrt_act(out, in_, bias, reads, out_tok, scale=1.0):
            A("act", reads, [out_tok], S.activation, out=out, in_=in_, func=AF.Ln, bias=bias, scale=scale)
            A("act", [out_tok], [out_tok], S.activation, out=out, in_=out, func=AF.Exp, scale=-0.5)

        last = None
        for (dst, src) in ((gains[:], gains_d), (convw[:], convw_d), (valid_sb[:], valid_d), (consts[:], consts_d)):
            last = DMA("sp", par_ch, [], [], dst, src)
        par_tok.writer = last
        A("dve", [par_tok], [cst_tok], V.tensor_copy, out=ident_bf[:], in_=consts[:, 0:128])
        A("dve", [par_tok], [cst_tok], V.tensor_copy, out=bdiag_bf[:], in_=consts[:, 128:256])
        A("dve", [par_tok], [cst_tok], V.tensor_copy, out=sel_bf[:], in_=consts[:, 1024:1152])
        A("pool", [], [cst_tok], G.memset, pat[:], 0.0)
        A("pool", [], [cst_tok], G.memset, pat[:, 0:1], 1.0)
        A("pool", [], [cst_tok], G.memset, pat[:, 2:3], 1.0)
        A("pool", [], [cst_tok], G.memset, ones_bf[:], 1.0 / 1024.0)
        A("pool", [], [cst_tok], G.memset, one64_bf[:], 1.0)
        for c in range(2):
            A("pool", [], [zb_tok[c]], G.memset, zb[:, c, :], 0.0)

        wconv_last = [[None] * 12 for _ in range(depth)]

        def grp_of(pi):
            return pi // 2

        def emit_wconv(l, plist=None):
            for pi, (wn, c0, ncol, kch) in enumerate(pcs):
                if plist is not None and pi not in plist:
                    continue
                src = wsrc[wn][l, :, c0:c0 + ncol].rearrange("(kc p) n -> p kc n", p=128)
                dst = wq_d[l, pi, :, 0:kch * ncol].rearrange("p (kc n) -> p kc n", kc=kch)
                g = grp_of(pi)
                wconv_last[l][g] = DMA("pool", wcv_ch[l][g], [], [], dst, src)

        emit_wconv(0, list(range(8)))

        slot_rr = [0, 0]

        def load_piece(l, pi):
            st = 0 if pi < 8 else 1
            s = 2 * st + slot_rr[st] % 2
            slot_rr[st] += 1
            wn, c0, ncol, kch = pcs[pi]
            n = kch * ncol
            DMA("sp", wslot_ch[s], [], [wslot_tok[s]], wslot[s][:, 0:n], wq_d[l, pi, :, 0:n],
                extra=[wconv_last[l][grp_of(pi)]])
            return wslot[s][:, 0:n].rearrange("p (kc n) -> p kc n", kc=kch), wslot_tok[s]

        def gcol(kind, l, c):
            base = (kind * depth + l) * 8 if kind < 3 else 3 * depth * 8
            return gains[:, base + c: base + c + 1]

        def layer_tables(l):
            for h in range(8):
                DMA("sp", bstage_ch, [], [bstage_tok], bstage[:], btab_d[l, h])
                A("dve", [bstage_tok, par_tok], [btab_tok[h]], V.tensor_tensor, out=btab[:, h, :], in0=bstage[:],
                  in1=consts[:, 384:1024], op=ALU.add)
            DMA("sp", wstage_ch, [], [wstage_tok], wstage[:, 0:512], sgwT_d[:, l * 512:(l + 1) * 512])
            for g in range(4):
                A("dve", [wstage_tok, par_tok], [wsT_tok], V.tensor_tensor, out=wsT[:, g, :],
                  in0=wstage[:, g * 128:(g + 1) * 128], in1=consts[:, 256:384], op=ALU.mult)
            last = None
            for (dst, src) in ((lng[:], lng_d[:, l * 256:(l + 1) * 256]), (lnb[:], lnb_d[:, l * 256:(l + 1) * 256]),
                               (sgbb[:], sgbb_d[:, l * 256:(l + 1) * 256].rearrange("p (c t) -> p c t", c=2))):
                last = DMA("sp", ltab_ch, [], [ltab_tok], dst, src)

        def rmsnorm(xb, kind, l, out_f32, pool="AB", dst=None, dst_tok=None, scratch=None):
            xtile = xt[xb]
            ps, pst = ps_alloc(pool)
            if scratch is None:
                sqb, sqt, rs, rs_tok = [sq[0][:], sq[1][:]], sq_tok, rstd[:], rstd_tok
            else:
                sqb, sqt, rs, rs_tok = scratch

            def sqr(c):
                s = c % 2
                if c % 2 == 0:
                    A("pool", [xt_tok[xb][c]], [sqt[s]], G.tensor_tensor, out=sqb[s], in0=xtile[:, c, :],
                      in1=xtile[:, c, :], op=ALU.mult)
                else:
                    A("dve", [xt_tok[xb][c]], [sqt[s]], V.tensor_tensor, out=sqb[s], in0=xtile[:, c, :],
                      in1=xtile[:, c, :], op=ALU.mult)
            sqr(0)
            for c in range(8):
                if c + 1 < 8:
                    sqr(c + 1)
                MM(ps[:], ones_bf[:], sqb[c % 2], c == 0, c == 7, [sqt[c % 2], cst_tok], [pst])
                yield 700.0
            rsqrt_act(rs, ps[:], EPS, [pst], rs_tok)
            yield 4000.0
            for c in range(8):
                if c == 4:
                    yield 3000.0
                if out_f32:
                    A("dve", [xt_tok[xb][c], rs_tok, par_tok], [xt_tok[xb][c]], V.scalar_tensor_tensor,
                      out=xtile[:, c, :], in0=xtile[:, c, :], scalar=gcol(kind, l, c), in1=rs,
                      op0=ALU.mult, op1=ALU.mult)
                else:
                    A("dve", [xt_tok[xb][c], rs_tok, par_tok], [dst_tok[c]], V.scalar_tensor_tensor,
                      out=dst[:, c, :], in0=xtile[:, c, :], scalar=gcol(kind, l, c), in1=rs,
                      op0=ALU.mult, op1=ALU.mult)

        gn_rr = [0]

        def gnorm_p1(src, src_toks, den, den_tok):
            i = gn_rr[0] % 2
            gn_rr[0] += 1
            A("act", src_toks, [sqE_tok[i]], S.activation, out=sqE[i][:], in_=src, func=AF.Square)
            if den is not None:
                dbank, pb = den
                A("act", [den_tok], [d2_tok], S.activation, out=d2r[pb:pb + 2, :], in_=dbank[pb:pb + 2, :],
                  func=AF.Square, scale=1e-3)
            return i

        def gnorm_p2(i, src, src_toks, den, l, mc, pool):
            ps, pst = ps_alloc(pool)
            if den is not None:
                dbank, pb = den
                MM(ps[:], bdiag_bf[:], sqE[i][:], True, False, [sqE_tok[i], cst_tok], [pst])
                for hh in range(2):
                    MM(ps[hh * 64:(hh + 1) * 64, :], sel_bf[pb:pb + 2, hh * 64:(hh + 1) * 64], d2r[pb:pb + 2, :],
                       False, True, [d2_tok, cst_tok], [pst])
                rsqrt_act(t32[i][:], ps[:], 1e-18, [pst], t32_tok[i])
            else:
                MM(ps[:], bdiag_bf[:], sqE[i][:], True, True, [sqE_tok[i], cst_tok], [pst])
                rsqrt_act(t32[i][:], ps[:], EPS, [pst], t32_tok[i])
            A("dve", src_toks + [t32_tok[i], par_tok], [mix_tok[mc]], V.scalar_tensor_tensor, out=mixT[:, mc, :],
              in0=src, scalar=gcol(1, l, mc), in1=t32[i][:], op0=ALU.mult, op1=ALU.mult)

        pt_rr = [0]
        y_rr = [0]
        r_rr = [0]

        def mmcost(n, cols):
            return n * max(64.0, cols * 0.52)

        def front(l, ti, xb, full):
            half = ti % 2
            src = xT_d if l == 0 else xs_d
            rd = [] if l == 0 else [xs_tok[ti]]
            DMA("sp", xld_ch[xb], rd, xt_tok[xb], xt[xb][:],
                src[:, ti * TT:(ti + 1) * TT].rearrange("(c p) t -> p c t", p=128))
            yield 20000.0
            yield from rmsnorm(xb, 0, l, False, "AB", nT, nT_tok)

            def fm_group(slot, stok, oc):
                ps, pst = ps_alloc("AB")
                for kc in range(8):
                    MM(ps[:], slot[:, kc, oc * 128:(oc + 1) * 128], nT[:, kc, :], kc == 0, kc == 7,
                       [stok, nT_tok[kc]], [pst])
                return ps, pst

            if full:
                slot, stok = load_piece(l, 0)
                for oc in range(4):
                    ps, pst = fm_group(slot, stok, oc)
                    A("act", [pst], [qT_tok[oc]], S.mul, out=qT[:, oc, :], in_=ps[:], mul=0.125)
                    yield mmcost(8, 512)
            slot, stok = load_piece(l, 1)
            for oc in range(4):
                ps, pst = fm_group(slot, stok, oc)
                A("act", [pst], [kT_tok[oc][half]], S.copy, out=kT[:, oc, half * TT:(half + 1) * TT], in_=ps[:])
                yield mmcost(8, 512)
            slot, stok = load_piece(l, 2)
            for oc in range(4):
                if oc < 2 and not full:
                    continue
                ps, pst = fm_group(slot, stok, oc)
                if oc < 2:
                    A("act", [pst], [bg_tok[oc]], S.copy, out=bg32[:, oc, :], in_=ps[:])
                else:
                    A("act", [pst], [cg_tok[oc - 2]], S.copy, out=cg32[:, oc - 2, :], in_=ps[:])
                yield mmcost(8, 512)
            slot, stok = load_piece(l, 3)
            for oc in range(4):
                if oc >= 2 and not full:
                    continue
                ps, pst = fm_group(slot, stok, oc)
                if oc < 2:
                    A("dve", [pst, cg_tok[oc]], [zb_tok[oc]], V.tensor_tensor, out=zb[:, oc, 2:TT + 2], in0=ps[:],
                      in1=cg32[:, oc, :], op=ALU.mult)
                else:
                    A("act", [pst], [su_tok[oc - 2]], S.activation, out=su32[:, oc - 2, :], in_=ps[:], func=AF.Gelu)
                yield mmcost(8, 512)
            slot, stok = load_piece(l, 4)
            for blk in range(4):
                ps, pst = ps_alloc("AB")
                for kc in range(8):
                    MM(ps[:], nT[:, kc, blk * 128:(blk + 1) * 128], slot[:, kc, :], kc == 0, kc == 7,
                       [stok, nT_tok[kc]], [pst])
                vb = half * 4 + blk
                A("act", [pst], [vtm_tok[vb]], S.copy, out=vtm[:, vb, :], in_=ps[:])
                A("pool", [par_tok, cst_tok], [vbc_tok[vb]], G.tensor_scalar, out=vbc[:, vb, :], in0=pat[:],
                  scalar1=valid_sb[:, ti * 4 + blk: ti * 4 + blk + 1], scalar2=None, op0=ALU.mult)
                yield mmcost(8, 512)
            if not full:
                for c in range(2):
                    A("dve", [zb_tok[c]], [zb_tok[c]], V.tensor_copy, out=zb[:, c, 0:2], in_=zb[:, c, TT:TT + 2])
                return
            slot, stok = load_piece(l, 5)
            for blk in range(4):
                ps, pst = ps_alloc("AB")
                for kc in range(8):
                    MM(ps[:, 0:256], nT[:, kc, blk * 128:(blk + 1) * 128], slot[:, kc, :], kc == 0, kc == 7,
                       [stok, nT_tok[kc]], [pst])
                A("act", [pst], [svg_tok[blk]], S.activation, out=svg[:, blk, :], in_=ps[:, 0:256], func=AF.Gelu)
                st = lnst[:, blk, :]
                lt = lnst_tok[blk]
                A("dve", [svg_tok[blk]], [lt], V.bn_stats, out=st[:, 0:6], in_=svg[:, blk, :])
                A("dve", [lt], [lt], V.bn_aggr, out=st[:, 6:8], in_=st[:, 0:6])
                A("act", [lt], [lt], S.activation, out=st[:, 8:9], in_=st[:, 7:8], func=AF.Ln, bias=EPS, scale=1.0)
                A("act", [lt], [lt], S.activation, out=st[:, 9:10], in_=st[:, 8:9], func=AF.Exp, scale=-0.5)
                A("dve", [lt, svg_tok[blk]], [svg_tok[blk]], V.tensor_scalar, out=svg[:, blk, :], in0=svg[:, blk, :],
                  scalar1=st[:, 6:7], scalar2=st[:, 9:10], op0=ALU.subtract, op1=ALU.mult)
                A("pool", [svg_tok[blk], ltab_tok], [svg_tok[blk]], G.tensor_tensor, out=svg[:, blk, :],
                  in0=svg[:, blk, :], in1=lng[:], op=ALU.mult)
                A("pool", [svg_tok[blk], ltab_tok], [vln_tok[blk]], G.tensor_tensor, out=vln[:, blk, :],
                  in0=svg[:, blk, :], in1=lnb[:], op=ALU.add)
                yield mmcost(8, 256) + 2000.0

            steps = [(c, hh, j) for c in range(4) for hh in range(2) for j in range(8)]
            accs = {}

            def geom(j):
                b_lo, b_hi = max(0, j - 4), min(3, j)
                nq = 128 * (b_hi - b_lo + 1)
                kh = (half + 1) % 2 if j < 4 else half
                return b_lo, nq, b_lo - j + 4, kh, j % 4

            def emit_S(k):
                c, hh, j = steps[k]
                h = 2 * c + hh
                p0, p1 = hh * 64, (hh + 1) * 64
                b_lo, nq, e_lo, kh, kb = geom(j)
                sps, spt = ps_alloc("A")
                MM(sps[:, 0:nq], kT[p0:p1, c, kh * TT + kb * 128: kh * TT + (kb + 1) * 128],
                   qT[p0:p1, c, b_lo * 128: b_lo * 128 + nq], True, False,
                   [kT_tok[c][kh], qT_tok[c]], [spt])
                MM(sps[:, 0:nq], ident_bf[:], btab[:, h, e_lo * 128: e_lo * 128 + nq], False, True,
                   [cst_tok, btab_tok[h]], [spt])
                pi = pt_rr[0] % NPT
                pt_rr[0] += 1
                A("act", [spt], [PT_tok[pi]], S.activation, out=PT[pi][:, 0:nq], in_=sps[:, 0:nq], func=AF.Exp)
                return pi

            def emit_PV(k, pi):
                c, hh, j = steps[k]
                h = 2 * c + hh
                p0, p1 = hh * 64, (hh + 1) * 64
                b_lo, nq, e_lo, kh, kb = geom(j)
                if (hh, j) == (0, 0):
                    accs[c] = ps_alloc("B")
                num, num_t = accs[c]
                den, den_t, pb = psum[5], den_tok4[c % 2], 32 * (c % 2)
                vb = kh * 4 + kb
                q0 = b_lo * 128
                MM(num[p0:p1, q0:q0 + nq], vtm[:, vb, h * 64:(h + 1) * 64], PT[pi][:, 0:nq],
                   j == 0, j == 7, [vtm_tok[vb], PT_tok[pi]], [num_t], skip=True)
                MM(den[pb:pb + 2, q0:q0 + nq], vbc[:, vb, hh:hh + 2], PT[pi][:, 0:nq],
                   (hh, j) == (0, 0), (hh, j) == (1, 7), [vbc_tok[vb], PT_tok[pi]], [den_t], skip=True)
                if (hh, j) == (1, 7):
                    gi = gnorm_p1(num[:], [num_t], (den, pb), den_t)
                    return (lambda gi=gi, num=num, num_t=num_t, den=den, pb=pb, c=c:
                            gnorm_p2(gi, num[:], [num_t], (den, pb), l, c, "A"))
                return None

            DEPTH_SW = 2
            GN_DEFER = 3
            pend = []
            pis = [emit_S(k) for k in range(DEPTH_SW)]
            for k in range(len(steps)):
                if k + DEPTH_SW < len(steps):
                    pis.append(emit_S(k + DEPTH_SW))
                p2 = emit_PV(k, pis[k])
                extra_c = 0.0
                if p2 is not None:
                    pend.append((k + GN_DEFER, p2))
                if pend and k >= pend[0][0]:
                    pend.pop(0)[1]()
                    extra_c = 3000.0
                yield mmcost(4, geom(steps[k][2])[1]) + extra_c

            def conv_ew(c):
                i = y_rr[0] % 2
                y_rr[0] += 1
                cw = lambda j: convw[:, (l * 3 + j) * 2 + c: (l * 3 + j) * 2 + c + 1]
                A("pool", [zb_tok[c], par_tok], [y32_tok[i]], G.tensor_scalar, out=y32[i][:], in0=zb[:, c, 2:TT + 2],
                  scalar1=cw(2), scalar2=None, op0=ALU.mult)
                A("dve", [zb_tok[c], y32_tok[i], par_tok], [y32_tok[i]], V.scalar_tensor_tensor, out=y32[i][:],
                  in0=zb[:, c, 1:TT + 1], scalar=cw(1), in1=y32[i][:], op0=ALU.mult, op1=ALU.add)
                A("dve", [zb_tok[c], y32_tok[i], par_tok], [y32_tok[i]], V.scalar_tensor_tensor, out=y32[i][:],
                  in0=zb[:, c, 0:TT], scalar=cw(0), in1=y32[i][:], op0=ALU.mult, op1=ALU.add)
                A("pool", [bg_tok[c], y32_tok[i]], [y32_tok[i]], G.tensor_tensor, out=y32[i][:], in0=y32[i][:],
                  in1=bg32[:, c, :], op=ALU.mult)
                A("dve", [zb_tok[c]], [zb_tok[c]], V.tensor_copy, out=zb[:, c, 0:2], in_=zb[:, c, TT:TT + 2])
                gi = gnorm_p1(y32[i][:], [y32_tok[i]], None, None)
                return lambda: gnorm_p2(gi, y32[i][:], [y32_tok[i]], None, l, 4 + c, "AB")

            def sg_ew(c2):
                ps, pst = ps_alloc("AB")
                for blk in range(4):
                    for gg in range(2):
                        g = 2 * c2 + gg
                        MM(ps[gg * 64:(gg + 1) * 64, blk * 128:(blk + 1) * 128], vln[:, blk, g * 64:(g + 1) * 64],
                           wsT[:, g, :], True, True, [vln_tok[blk], wsT_tok], [pst])
                i = y_rr[0] % 2
                y_rr[0] += 1
                for blk in range(4):
                    A("dve", [pst, ltab_tok], [y32_tok[i]], V.tensor_tensor, out=y32[i][:, blk * 128:(blk + 1) * 128],
                      in0=ps[:, blk * 128:(blk + 1) * 128], in1=sgbb[:, c2, :], op=ALU.add)
                A("pool", [su_tok[c2], y32_tok[i]], [y32_tok[i]], G.tensor_tensor, out=y32[i][:], in0=y32[i][:],
                  in1=su32[:, c2, :], op=ALU.mult)
                gi = gnorm_p1(y32[i][:], [y32_tok[i]], None, None)
                return lambda: gnorm_p2(gi, y32[i][:], [y32_tok[i]], None, l, 6 + c2, "AB")

            q2 = [p[1] for p in pend]
            for seg in (lambda: conv_ew(0), lambda: conv_ew(1), lambda: sg_ew(0), lambda: sg_ew(1)):
                q2.append(seg())
                yield 3000.0
                if len(q2) > 1:
                    q2.pop(0)()
                    yield 3000.0
            while q2:
                q2.pop(0)()
                yield 3000.0

            for p2 in range(2):
                slot, stok = load_piece(l, 6 + p2)
                for oc in range(4):
                    ps, pst = ps_alloc("AB")
                    for kc in range(8):
                        MM(ps[:], slot[:, kc, oc * 128:(oc + 1) * 128], mixT[:, kc, :], kc == 0, kc == 7,
                           [stok, mix_tok[kc]], [pst])
                    xc = p2 * 4 + oc
                    A("dve", [pst, xt_tok[xb][xc]], [xt_tok[xb][xc]], V.tensor_tensor, out=xt[xb][:, xc, :],
                      in0=ps[:], in1=xt[xb][:, xc, :], op=ALU.add)
                    yield mmcost(8, 512)
            yield from rmsnorm(xb, 2, l, False, "AB", nT2, nT2_tok)

        def ffn(l, ti, xb):
            last_layer = (l == depth - 1)
            for p8 in range(8):
                slot, stok = load_piece(l, 8 + p8)
                for oc in range(4):
                    ps, pst = ps_alloc("G")
                    for kc in range(8):
                        MM(ps[:], slot[:, kc, oc * 128:(oc + 1) * 128], nT2[:, kc, :], kc == 0, kc == 7,
                           [stok, nT2_tok[kc]], [pst])
                    i = r_rr[0] % 2
                    r_rr[0] += 1
                    fc = p8 * 4 + oc
                    A("dve", [pst], [r32_tok[i]], V.tensor_scalar, out=r32[i][:], in0=ps[:], scalar1=0.0, scalar2=None,
                      op0=ALU.max)
                    A("dve", [r32_tok[i]], [hdn_tok[fc]], V.tensor_tensor, out=hdnT[:, fc, :], in0=r32[i][:],
                      in1=r32[i][:], op=ALU.mult)
                    yield mmcost(8, 512)
            for oc in range(8):
                slot, stok = load_piece(l, 16 + oc)
                ps, pst = ps_alloc("G")
                for kc in range(32):
                    MM(ps[:], slot[:, kc, :], hdnT[:, kc, :], kc == 0, kc == 31, [stok, hdn_tok[kc]], [pst])
                    if kc % 8 == 7 and kc != 31:
                        yield mmcost(8, 512)
                A("dve", [pst, xt_tok[xb][oc]], [xt_tok[xb][oc]], V.tensor_tensor, out=xt[xb][:, oc, :],
                  in0=ps[:], in1=xt[xb][:, oc, :], op=ALU.add)
                yield mmcost(8, 512)
            if last_layer:
                yield from rmsnorm(xb, 3, l, True, "G", scratch=([hdnT[:, 0, :], hdnT[:, 1, :]], [hdn_tok[0], hdn_tok[1]],
                                                               r32[0][:], r32_tok[0]))
                to = ti - depth
                DMA("sp", xst_ch[xb], xt_tok[xb], [], outT_d[:, to * TT:(to + 1) * TT].rearrange("(c p) t -> p c t", p=128),
                    xt[xb][:])
            else:
                DMA("sp", xst_ch[xb], xt_tok[xb], [xs_tok[ti]],
                    xs_d[:, ti * TT:(ti + 1) * TT].rearrange("(c p) t -> p c t", p=128), xt[xb][:])
            yield mmcost(8, 512)

        def total_cost(mk):
            save = (dict(rr), list(slot_rr), pt_rr[0], y_rr[0], r_rr[0], gn_rr[0])
            dry[0] = True
            tot = sum(mk())
            dry[0] = False
            rr.clear()
            rr.update(save[0])
            slot_rr[:] = save[1]
            pt_rr[0], y_rr[0], r_rr[0], gn_rr[0] = save[2:]
            return tot

        def merge(mka, mkb):
            if mkb is None:
                for _ in mka():
                    pass
                return
            if mka is None:
                for _ in mkb():
                    pass
                return
            ta, tb = total_cost(mka), total_cost(mkb)
            ga, gb = mka(), mkb()
            ca = cb = 0.0
            da = db = False
            while not (da and db):
                pick_a = (not da) and (db or ca / ta <= cb / tb)
                if pick_a:
                    try:
                        ca += next(ga)
                    except StopIteration:
                        da = True
                else:
                    try:
                        cb += next(gb)
                    except StopIteration:
                        db = True

        xb = 0
        pending = None
        for l in range(depth):
            for ti in range(l, W):
                if ti == l:
                    if PIPELINE and pending is not None:
                        pass
                    layer_tables(l)
                if l + 1 < depth:
                    per = -(-NPIECE // max(1, W - l - 1))
                    k0 = (ti - l) * per
                    emit_wconv(l + 1, list(range(k0, min(k0 + per, NPIECE))))
                full = ti > l
                mk_front = (lambda l=l, ti=ti, xb=xb, full=full: front(l, ti, xb, full))
                if PIPELINE:
                    merge(mk_front, pending)
                else:
                    merge(mk_front, None)
                    if pending is not None:
                        merge(pending, None)
                    pending = None
                if l == 0 and ti == 0:
                    emit_wconv(0, list(range(8, NPIECE)))
                pending = (lambda l=l, ti=ti, xb=xb: ffn(l, ti, xb)) if full else None
                if not PIPELINE and pending is not None:
                    merge(pending, None)
                    pending = None
                xb ^= 1
        if pending is not None:
            merge(pending, None)

        cnt = P.emit(sems)
        for c in xst_ch:
            if c.count:
                nc.sync.wait_ge(c.sem, c.count)
        print("[kernel] instrs=%d sem counts=%s" % (len(P.ins), cnt))
    return nc


def host_prep(inputs, depth, n_out_tiles, ncore_per_seq):
    x = np.asarray(inputs["x"], dtype=np.float32)
    B, SEQ, _ = x.shape
    TPC = n_out_tiles * TT
    assert TPC * ncore_per_seq == SEQ
    halo = depth * TT
    NW = TPC + halo
    NB = NW // 128
    f = lambda k: np.asarray(inputs[k], dtype=np.float32)

    def fm(v):
        v = v.reshape(-1, 8, 128)
        return np.ascontiguousarray(v.transpose(2, 0, 1)).reshape(128, -1)

    gains = np.concatenate([fm(f("mix_norm_g")), fm(f("group_norm_g")), fm(f("mlp_norm_g")),
                            fm(f("final_norm_g")[None])], axis=1)
    cw = f("conv_w").reshape(depth, 3, 2, 128)
    convw = np.ascontiguousarray(cw.transpose(3, 0, 1, 2)).reshape(128, depth * 6)
    lng = np.ascontiguousarray(np.broadcast_to(f("sg_ln_g").reshape(1, depth * 256), (128, depth * 256)))
    lnb = np.ascontiguousarray(np.broadcast_to(f("sg_ln_b").reshape(1, depth * 256), (128, depth * 256)))
    sb_ = f("sg_b")
    t1 = sb_.reshape(depth, 2, 2, 1, 128)
    t1 = np.broadcast_to(t1, (depth, 2, 2, 64, 128))
    sgbb = np.ascontiguousarray(t1.transpose(2, 3, 0, 1, 4)).reshape(128, depth * 2 * 128)
    sgwT = np.ascontiguousarray(f("sg_w").transpose(3, 0, 1, 2)).reshape(128, depth * 4 * 128)
    ident = np.eye(128, dtype=np.float32)
    bd = np.zeros((128, 128), np.float32)
    bd[:64, :64] = 1.0 / 64
    bd[64:, 64:] = 1.0 / 64
    triu = np.triu(np.ones((128, 128), np.float32))
    kk = np.arange(128)[:, None]
    cc = np.arange(640)[None, :]
    maskc = np.zeros((128, 640), np.float32)
    maskc[(kk >= 64) & (cc < 64)] = -30000.0
    maskc[(kk < 64) & (cc >= 576)] = -30000.0
    sel = np.zeros((128, 128), np.float32)
    sel[np.arange(128) % 32 == 0, 0:64] = 1.0
    sel[np.arange(128) % 32 == 1, 64:128] = 1.0
    consts = np.concatenate([ident, bd, triu, maskc, sel], axis=1)
    idx = np.clip(cc - kk, -128, 128) + 128
    btab = np.ascontiguousarray(f("rel_bias")[:, :, idx])
    shared = {"w_in": f("w_in"), "w_out": f("w_out"), "w_up": f("w_up"), "w_down": f("w_down"),
              "gains": gains, "convw": convw, "lng": lng, "lnb": lnb, "sgbb": sgbb, "sgwT": sgwT,
              "consts": consts, "btab": btab}
    in_maps = []
    for core in range(B * ncore_per_seq):
        b, q = divmod(core, ncore_per_seq)
        s0 = q * TPC - halo
        win = np.zeros((NW, D), np.float32)
        lo = max(s0, 0)
        win[lo - s0:] = x[b, lo:(q + 1) * TPC]
        val = np.zeros((NW,), np.float32)
        val[lo - s0:] = 1.0
        m = dict(shared)
        m["xT"] = np.ascontiguousarray(win.T)
        m["valid"] = np.ascontiguousarray(val.reshape(NB, 128).T)
        in_maps.append(m)
    return in_maps


def run(inputs, depth, n_out_tiles, ncore_per_seq):
    x = inputs["x"]
    B, SEQ, _ = x.shape
    in_maps = host_prep(inputs, depth, n_out_tiles, ncore_per_seq)
    nc = build(depth, n_out_tiles)
    n = len(in_maps)
    res = run_bass_kernel_spmd(nc, in_maps, core_ids=list(range(n)))
    TPC = n_out_tiles * TT
    out = np.empty((B, SEQ, D), np.float32)
    for core in range(n):
        b, q = divmod(core, ncore_per_seq)
        out[b, q * TPC:(q + 1) * TPC] = res.results[core]["outT"].T
    return out


def kernel(**inputs):
    return run(inputs, depth=4, n_out_tiles=8, ncore_per_seq=4)
```
